# Optimizing a Trainium2 kernel written in Bass

```python
import jax, jax.numpy as jnp
from jax import lax
import numpy as np

D_MODEL = 2048
BATCH = 2
SEQ = 4096
DEPTH = 4
DEC_BATCH = 8
DEC_SEQ = 8
PAST_LEN = 16384
PAGE_SIZE = 128

N_MIXERS = 3
D_FF = 4 * D_MODEL
RMS_EPS = 1e-6
LN_EPS = 1e-5
CONV_WIDTH = 31
D_CONV = D_MODEL
N_CONV_LAYERS = len(range(0, DEPTH, N_MIXERS))
D_RNN = D_MODEL
LRU_BLOCK = 256
N_LRU_BLOCKS = D_RNN // LRU_BLOCK
LRU_CONV_WIDTH = 4
LRU_C = 8.0
HEAD_DIM = 128
HEADS_PER_GROUP = 8
GROUPS = ((128, 1), (512, 4), (2048, 16))
N_GROUPS = len(GROUPS)
N_ATT_HEADS = N_GROUPS * HEADS_PER_GROUP
ATT_WIDTH = HEADS_PER_GROUP * HEAD_DIM
N_DIL_KEYS = GROUPS[0][0] // GROUPS[0][1] + 1
BLK = N_DIL_KEYS - 1
N_BUCKETS = 32
MAX_DISTANCE = 2048
NEG_INF = -1e30

kernel_name = 'hybrid_conformer_rglru_dilated_attn_step'


def _rms_norm(x, g):
    xf = x.astype(jnp.float32)
    y = xf * lax.rsqrt(jnp.mean(xf * xf, axis=-1, keepdims=True) + RMS_EPS)
    return (y * g.astype(jnp.float32)).astype(x.dtype)


def _layer_norm(x, g, b):
    xf = x.astype(jnp.float32)
    mu = jnp.mean(xf, axis=-1, keepdims=True)
    var = jnp.mean(jnp.square(xf - mu), axis=-1, keepdims=True)
    y = (xf - mu) * lax.rsqrt(var + LN_EPS)
    return (y * g.astype(jnp.float32) + b.astype(jnp.float32)).astype(x.dtype)


def _sq_relu_mlp(x, w1, w2):
    return jnp.square(jax.nn.relu(x @ w1)) @ w2


def _causal_depthwise(x_ext, w, b):
    c = x_ext.shape[-1]
    y = lax.conv_general_dilated(x_ext, w[:, None, :].astype(x_ext.dtype), window_strides=(1,),
                                 padding='VALID', dimension_numbers=('NWC', 'WIO', 'NWC'),
                                 feature_group_count=c)
    return y + b


def _conv_module(h, buf, w_glu, dw_w, dw_b, ln_g, ln_b, w_out):
    a, gate = jnp.split(h @ w_glu, 2, axis=-1)
    u = a * jax.nn.sigmoid(gate)
    u_ext = jnp.concatenate([buf, u], axis=1)
    c = _causal_depthwise(u_ext, dw_w, dw_b)
    c = jax.nn.silu(_layer_norm(c, ln_g, ln_b))
    return c @ w_out, u_ext[:, -(CONV_WIDTH - 1):]


def _rglru_block(h, h0, buf, w_in, conv_w, conv_b, w_a, b_a, w_x, b_x, lam, w_out):
    B, L, _ = h.shape
    y_br, x_br = jnp.split(h @ w_in, 2, axis=-1)
    y_br = jax.nn.gelu(y_br, approximate=True)
    x_ext = jnp.concatenate([buf, x_br], axis=1)
    xc = _causal_depthwise(x_ext, conv_w, conv_b)
    xb = xc.reshape(B, L, N_LRU_BLOCKS, LRU_BLOCK)
    r = jax.nn.sigmoid(jnp.einsum('blnc,ncd->blnd', xb, w_a).reshape(B, L, D_RNN) + b_a)
    i = jax.nn.sigmoid(jnp.einsum('blnc,ncd->blnd', xb, w_x).reshape(B, L, D_RNN) + b_x)
    log_a = -LRU_C * r.astype(jnp.float32) * jax.nn.softplus(-lam.astype(jnp.float32))
    a = jnp.exp(log_a)
    b = jnp.sqrt(-jnp.expm1(2.0 * log_a)) * (i * xc).astype(jnp.float32)

    def step(hc, ab):
        hc = ab[0] * hc + ab[1]
        return hc, hc

    h_last, hs = lax.scan(step, h0.astype(jnp.float32), (jnp.swapaxes(a, 0, 1), jnp.swapaxes(b, 0, 1)))
    hs = jnp.swapaxes(hs, 0, 1).astype(h.dtype)
    return (hs * y_br) @ w_out, h_last.astype(h0.dtype), x_ext[:, -(LRU_CONV_WIDTH - 1):]


def _t5_bucket(dist):
    max_exact = N_BUCKETS // 2
    d = np.asarray(dist, dtype=np.int32)
    large = max_exact + (np.log(np.maximum(d, max_exact) / max_exact) / np.log(MAX_DISTANCE / max_exact)
                         * (N_BUCKETS - max_exact)).astype(np.int32)
    return np.where(d < max_exact, d, np.minimum(large, N_BUCKETS - 1)).astype(np.int32)


def _group_bias(rel_bias, g):
    dil = GROUPS[g][1]
    buckets = _t5_bucket(np.arange(N_DIL_KEYS) * dil)
    tab = rel_bias[:, g * HEADS_PER_GROUP:(g + 1) * HEADS_PER_GROUP].astype(jnp.float32)
    return tab[buckets]


def _dilated_prompt(q, k, v, dil, bias):
    B, S, H, Dh = q.shape
    unit = dil * BLK
    Sp = -(-S // unit) * unit
    M = Sp // dil
    nb = M // BLK

    def strided(t):
        t = jnp.pad(t, ((0, 0), (0, Sp - S), (0, 0), (0, 0)))
        return t.reshape(B, M, dil, H, Dh).transpose(0, 2, 1, 3, 4).reshape(B, dil, nb, BLK, H, Dh)

    def with_prev(t):
        prev = jnp.pad(t[:, :, :-1], ((0, 0), (0, 0), (1, 0), (0, 0), (0, 0), (0, 0)))
        return jnp.concatenate([prev, t], axis=3)

    qs = strided(q)
    kb = with_prev(strided(k))
    vb = with_prev(strided(v))
    s = jnp.einsum('brnqhe,brnkhe->brnhqk', qs, kb).astype(jnp.float32) * (Dh ** -0.5)
    qi = np.arange(BLK)[:, None]
    kj = np.arange(2 * BLK)[None, :]
    delta = BLK + qi - kj
    key_idx = np.arange(nb)[:, None, None] * BLK - BLK + kj[None]
    valid = (delta >= 0)[None] & (delta < N_DIL_KEYS)[None] & (key_idx >= 0)
    s = s + jnp.transpose(bias[np.clip(delta, 0, N_DIL_KEYS - 1)], (2, 0, 1))
    s = jnp.where(valid[:, None], s, NEG_INF)
    m = jnp.max(s, axis=-1, keepdims=True)
    p = jnp.exp(s - m)
    den = jnp.sum(p, axis=-1, keepdims=True)
    o = jnp.einsum('brnhqk,brnkhe->brnqhe', (p / den).astype(v.dtype), vb)
    lse = (m + jnp.log(den))[..., 0]
    o = o.reshape(B, dil, M, H, Dh).transpose(0, 2, 1, 3, 4).reshape(B, Sp, H, Dh)[:, :S]
    lse = lse.transpose(0, 1, 2, 4, 3).reshape(B, dil, M, H).transpose(0, 2, 1, 3).reshape(B, Sp, H)[:, :S]
    return o, lse


def _dilated_step(q, kv_all, dil, bias, n_buf):
    T, Dh = q.shape[1], q.shape[-1]
    idx = n_buf + np.arange(T)[:, None] - dil * np.arange(N_DIL_KEYS)[None, :]
    valid = idx >= 0
    kvg = kv_all[:, np.maximum(idx, 0)]
    s = jnp.einsum('bthe,btkhe->bhtk', q, kvg[:, :, :, 0]).astype(jnp.float32) * (Dh ** -0.5)
    s = s + bias.T[None, :, None, :]
    s = jnp.where(valid, s, NEG_INF)
    m = jnp.max(s, axis=-1, keepdims=True)
    p = jnp.exp(s - m)
    den = jnp.sum(p, axis=-1, keepdims=True)
    o = jnp.einsum('bhtk,btkhe->bthe', (p / den).astype(q.dtype), kvg[:, :, :, 1])
    lse = jnp.transpose((m + jnp.log(den))[..., 0], (0, 2, 1))
    return o, lse


def _dilated_attention(h, w_qkv, w_o, rel_bias, bufs):
    B, L, _ = h.shape
    qkv = (h @ w_qkv).reshape(B, L, 3, N_GROUPS, HEADS_PER_GROUP, HEAD_DIM)
    outs, lses, rows = [], [], []
    for g, (win, dil) in enumerate(GROUPS):
        q = qkv[:, :, 0, g]
        kv = qkv[:, :, 1:, g]
        bias = _group_bias(rel_bias, g)
        if bufs is None:
            o, lse = _dilated_prompt(q, kv[:, :, 0], kv[:, :, 1], dil, bias)
            rows.append(kv[:, -min(win, L):])
        else:
            buf = bufs[g]
            o, lse = _dilated_step(q, jnp.concatenate([buf, kv], axis=1), dil, bias, buf.shape[1])
            rows.append(kv)
        outs.append(o)
        lses.append(lse)
    wts = jax.nn.softmax(jnp.stack(lses), axis=0)
    o = jnp.sum(wts[..., None] * jnp.stack(outs).astype(jnp.float32), axis=0).astype(h.dtype)
    return o.reshape(B, L, ATT_WIDTH) @ w_o, rows


def setup_inputs(seed: int = 0) -> dict:
    key = jax.random.key(seed)
    keys = jax.random.split(key, 40)
    counter = [0]

    def nk():
        counter[0] += 1
        return keys[counter[0] - 1]

    def nrm(shape, scale):
        return jax.random.normal(nk(), shape, jnp.float32) * scale

    def gain(shape):
        return 1.0 + nrm(shape, 0.01)

    u = jax.random.uniform(nk(), (D_RNN,), jnp.float32, 0.9, 0.999) ** (1.0 / LRU_C)
    lru_lambda = jnp.log(u) - jnp.log1p(-u)
    kv_shape = lambda w: (DEC_BATCH, min(w, PAST_LEN), 2, HEADS_PER_GROUP, HEAD_DIM)
    return {
        'x_prompt': nrm((BATCH, SEQ, D_MODEL), 1.0),
        'x_sample': nrm((DEC_BATCH, DEC_SEQ, D_MODEL), 1.0),
        'state_conv_l0': nrm((DEC_BATCH, CONV_WIDTH - 1, D_CONV), 0.5),
        'state_lru_h_l1': nrm((DEC_BATCH, D_RNN), 0.3),
        'state_lru_conv_l1': nrm((DEC_BATCH, LRU_CONV_WIDTH - 1, D_RNN), 1.0),
        'cache_kv_w128_l2': nrm(kv_shape(GROUPS[0][0]), 1.0),
        'cache_kv_w512_l2': nrm(kv_shape(GROUPS[1][0]), 1.0),
        'cache_kv_w2048_l2': nrm(kv_shape(GROUPS[2][0]), 1.0),
        'state_conv_l3': nrm((DEC_BATCH, CONV_WIDTH - 1, D_CONV), 0.5),
        'norm_mix_g': gain((DEPTH, D_MODEL)),
        'norm_ffn_g': gain((DEPTH, D_MODEL)),
        'final_norm_g': gain((D_MODEL,)),
        'ffn_w1': nrm((DEPTH, D_MODEL, D_FF), D_MODEL ** -0.5),
        'ffn_w2': nrm((DEPTH, D_FF, D_MODEL), D_FF ** -0.5),
        'conv_w_glu': nrm((N_CONV_LAYERS, D_MODEL, 2 * D_CONV), D_MODEL ** -0.5),
        'conv_dw_w': nrm((N_CONV_LAYERS, CONV_WIDTH, D_CONV), CONV_WIDTH ** -0.5),
        'conv_dw_b': nrm((N_CONV_LAYERS, D_CONV), 0.01),
        'conv_ln_g': gain((N_CONV_LAYERS, D_CONV)),
        'conv_ln_b': nrm((N_CONV_LAYERS, D_CONV), 0.01),
        'conv_w_out': nrm((N_CONV_LAYERS, D_CONV, D_MODEL), D_CONV ** -0.5),
        'lru_w_in': nrm((D_MODEL, 2 * D_RNN), D_MODEL ** -0.5),
        'lru_conv_w': nrm((LRU_CONV_WIDTH, D_RNN), LRU_CONV_WIDTH ** -0.5),
        'lru_conv_b': nrm((D_RNN,), 0.01),
        'lru_w_a': nrm((N_LRU_BLOCKS, LRU_BLOCK, LRU_BLOCK), LRU_BLOCK ** -0.5),
        'lru_b_a': nrm((D_RNN,), 0.01),
        'lru_w_x': nrm((N_LRU_BLOCKS, LRU_BLOCK, LRU_BLOCK), LRU_BLOCK ** -0.5),
        'lru_b_x': nrm((D_RNN,), 0.01),
        'lru_lambda': lru_lambda,
        'lru_w_out': nrm((D_RNN, D_MODEL), D_RNN ** -0.5),
        'att_w_qkv': nrm((D_MODEL, 3 * N_GROUPS * ATT_WIDTH), D_MODEL ** -0.5),
        'att_w_o': nrm((ATT_WIDTH, D_MODEL), ATT_WIDTH ** -0.5),
        'rel_bias': nrm((N_BUCKETS, N_ATT_HEADS), 0.2),
    }


def reference(x_prompt, x_sample, state_conv_l0, state_lru_h_l1, state_lru_conv_l1,
              cache_kv_w128_l2, cache_kv_w512_l2, cache_kv_w2048_l2, state_conv_l3,
              norm_mix_g, norm_ffn_g, final_norm_g, ffn_w1, ffn_w2,
              conv_w_glu, conv_dw_w, conv_dw_b, conv_ln_g, conv_ln_b, conv_w_out,
              lru_w_in, lru_conv_w, lru_conv_b, lru_w_a, lru_b_a, lru_w_x, lru_b_x, lru_lambda, lru_w_out,
              att_w_qkv, att_w_o, rel_bias):

    def trunk(x, st):
        B = x.shape[0]
        new = []
        for i in range(DEPTH):
            kind = i % N_MIXERS
            h = _rms_norm(x, norm_mix_g[i])
            if kind == 0:
                j = i // N_MIXERS
                buf = jnp.zeros((B, CONV_WIDTH - 1, D_CONV), x.dtype) if st is None else st[i][0]
                y, nbuf = _conv_module(h, buf, conv_w_glu[j], conv_dw_w[j], conv_dw_b[j],
                                       conv_ln_g[j], conv_ln_b[j], conv_w_out[j])
                new.append((nbuf,))
            elif kind == 1:
                if st is None:
                    h0 = jnp.zeros((B, D_RNN), x.dtype)
                    cb = jnp.zeros((B, LRU_CONV_WIDTH - 1, D_RNN), x.dtype)
                else:
                    h0, cb = st[i]
                y, h_last, ncb = _rglru_block(h, h0, cb, lru_w_in, lru_conv_w, lru_conv_b, lru_w_a, lru_b_a,
                                              lru_w_x, lru_b_x, lru_lambda, lru_w_out)
                new.append((h_last, ncb))
            else:
                y, rows = _dilated_attention(h, att_w_qkv, att_w_o, rel_bias, None if st is None else st[i])
                new.append(rows)
            x = x + y
            x = x + _sq_relu_mlp(_rms_norm(x, norm_ffn_g[i]), ffn_w1[i], ffn_w2[i])
        return _rms_norm(x, final_norm_g), new

    y_prompt, pst = trunk(x_prompt, None)
    y_sample, sst = trunk(x_sample, [(state_conv_l0,), (state_lru_h_l1, state_lru_conv_l1),
                                     (cache_kv_w128_l2, cache_kv_w512_l2, cache_kv_w2048_l2),
                                     (state_conv_l3,)])
    return (y_prompt, y_sample,
            pst[0][0], sst[0][0],
            pst[1][0], sst[1][0],
            pst[1][1], sst[1][1],
            pst[2][0], sst[2][0],
            pst[2][1], sst[2][1],
            pst[2][2], sst[2][2],
            pst[3][0], sst[3][0])
```

```python
import numpy as np
import concourse.bass as bass
import concourse.mybir as mybir
from concourse.bass_utils import run_bass_kernel_spmd

F32 = mybir.dt.float32
BF16 = mybir.dt.bfloat16
AF = mybir.ActivationFunctionType
ALU = mybir.AluOpType

NCORES = 8
D = 2048
KC = 16
DFF = 8192
NPR = 1024
NSM = 8
NT = NPR + NSM
TT = [(0, 512), (512, 512), (1024, 8)]
CW = 31
BW = 256
CPB = BW // 128
RMS_EPS = 1e-6
LN_EPS = 1e-5
import os
ATT_STAGE = int(os.environ.get('ATT_STAGE', '9'))
START_LAYER = int(os.environ.get('START_LAYER', '0'))
ATT_SKIP = os.environ.get('ATT_SKIP', '')
NLAYERS = 4

PV = {}
def _pv_names():
    names = []
    for i in range(4): names.append(f"nmix{i}")
    for i in range(4): names.append(f"nffn{i}")
    names.append("nfinal")
    for j in range(2):
        for t in range(CW): names.append(f"cdw{j}_{t}")
        names += [f"cdb{j}", f"clg{j}", f"clb{j}"]
    for t in range(4): names.append(f"lcw{t}")
    names += ["lcb", "lba", "lbx", "llam"]
    return names
PV_NAMES = _pv_names()
for _i, _n in enumerate(PV_NAMES): PV[_n] = _i
NPV = len(PV_NAMES)


class Sync:
    def __init__(self, nc):
        self.nc = nc
        self.engs = {"pe": nc.tensor, "act": nc.scalar, "dve": nc.vector, "pool": nc.gpsimd, "sp": nc.sync}
        self.sem = {k: nc.alloc_semaphore(f"ms_{k}") for k in self.engs}
        self.cnt = {k: 0 for k in self.engs}
        self.seen = {k: {} for k in self.engs}
        self.dsem = {}
        self.dcnt = {}

    def mark(self, e, instr):
        self.cnt[e] += 1
        instr.then_inc(self.sem[e], 1)
        return ("e", e, self.cnt[e])

    def cur(self, e):
        return ("e", e, self.cnt[e])

    def wait(self, e, tok):
        if tok is None:
            return
        if isinstance(tok, list):
            for t in tok: self.wait(e, t)
            return
        kind, key, val = tok
        if val <= 0:
            return
        k2 = (kind, key)
        if self.seen[e].get(k2, 0) >= val:
            return
        self.seen[e][k2] = val
        sem = self.sem[key] if kind == "e" else self.dsem[key]
        self.engs[e].wait_ge(sem, val)

    def op(self, e, fn, deps=(), mark=True):
        for d in deps: self.wait(e, d)
        ins = fn(self.engs[e])
        return self.mark(e, ins) if mark else None

    def dma(self, e, out, in_, deps=(), sem="misc", **kw):
        for d in deps: self.wait(e, d)
        if sem not in self.dsem:
            self.dsem[sem] = self.nc.alloc_semaphore(f"ds_{sem}")
            self.dcnt[sem] = 0
        self.dcnt[sem] += 16
        self.engs[e].dma_start(out=out, in_=in_, **kw).then_inc(self.dsem[sem], 16)
        return ("d", sem, self.dcnt[sem])

    def barrier(self, extra=()):
        toks = [("e", k, self.cnt[k]) for k in self.engs] + [("d", k, v) for k, v in self.dcnt.items()]
        toks += list(extra)
        for e in self.engs:
            for t in toks:
                if not (t[0] == "e" and t[1] == e):
                    self.wait(e, t)


class WStream:
    def __init__(self, S, slots, plan):
        self.S, self.slots, self.plan = S, slots, plan
        self.ns = len(slots)
        self.issued = 0
        self.consumed = 0
        self.free_tok = [None] * self.ns
        self.load_tok = {}
        for _ in range(self.ns): self._issue()

    def _issue(self):
        if self.issued >= len(self.plan):
            return
        i = self.issued
        s = i % self.ns
        toks = []
        for (dst_fn, src) in self.plan[i]:
            toks.append(self.S.dma("pool", dst_fn(self.slots[s]), src, deps=[self.free_tok[s]], sem=f"w{s}"))
        self.load_tok[i] = toks[-1]
        self.issued += 1

    def acquire(self):
        i = self.consumed
        assert i < self.issued, "weight plan exhausted"
        return self.slots[i % self.ns], self.load_tok[i]

    def peek(self, k):
        i = self.consumed + k
        assert i < self.issued, "peek beyond issued"
        return self.slots[i % self.ns], self.load_tok[i]

    def release(self, pe_tok):
        i = self.consumed
        self.free_tok[i % self.ns] = pe_tok
        self.consumed += 1
        self._issue()


def build_program():
    nc = bass.Bass("TRN2", target_bir_lowering=False)
    S = Sync(nc)
    dt = nc.dram_tensor

    xT = dt("xT", [D, NT], F32, kind="ExternalInput").ap()
    pvec = dt("pvec", [128, NPV, KC], F32, kind="ExternalInput").ap()
    meta = dt("meta", [128, 16], F32, kind="ExternalInput").ap()
    sc0T = dt("sc0T", [D, 30], F32, kind="ExternalInput").ap() if START_LAYER == 0 else None
    sc3T = dt("sc3T", [D, 30], F32, kind="ExternalInput").ap() if (START_LAYER == 0 or NLAYERS == 4) else None
    LAYERS = list(range(START_LAYER, NLAYERS))
    ffn_w1 = {i: dt(f"ffn_w1_{i}", [D, DFF], F32, kind="ExternalInput").ap() for i in LAYERS}
    ffn_w2 = {i: dt(f"ffn_w2_{i}", [DFF, D], F32, kind="ExternalInput").ap() for i in LAYERS}
    conv_w_glu = {i // 3: dt(f"conv_w_glu_{i // 3}", [D, 2 * D], F32, kind="ExternalInput").ap() for i in LAYERS if i % 3 == 0}
    conv_w_out = {i // 3: dt(f"conv_w_out_{i // 3}", [D, D], F32, kind="ExternalInput").ap() for i in LAYERS if i % 3 == 0}

    HAS_LRU = 1 in LAYERS
    if HAS_LRU:
        lru_w_in = dt("lru_w_in", [D, 2 * D], F32, kind="ExternalInput").ap()
        lru_w_a = dt("lru_w_a", [8, 256, 256], F32, kind="ExternalInput").ap()
        lru_w_x = dt("lru_w_x", [8, 256, 256], F32, kind="ExternalInput").ap()
        lru_w_out = dt("lru_w_out", [D, D], F32, kind="ExternalInput").ap()
        lh0 = dt("lh0", [128, KC], F32, kind="ExternalInput").ap()
        lcsT = dt("lcsT", [D, 3], F32, kind="ExternalInput").ap()
        o_lhp = dt("o_lhp", [128, KC], F32, kind="ExternalOutput").ap()
        o_lhs = dt("o_lhs", [128, KC], F32, kind="ExternalOutput").ap()
        o_lcp = dt("o_lcp", [D, 3], F32, kind="ExternalOutput").ap()
        o_lcs = dt("o_lcs", [D, 3], F32, kind="ExternalOutput").ap()
        lc_in = dt("lc_in", [128, 32], F32)
        lc_out = dt("lc_out", [NCORES * 128, 32], F32)
    HAS_ATT = 2 in LAYERS
    if HAS_ATT:
        att_w_qkv = dt("att_w_qkv", [D, 9216], F32, kind="ExternalInput").ap()
        att_w_o = dt("att_w_o", [1024, D], F32, kind="ExternalInput").ap()
        rel_bias = dt("rel_bias", [32, 24], F32, kind="ExternalInput").ap()
        selc = dt("selc", [33, 3, 400], F32, kind="ExternalInput").ap()
        GW = [128, 512, 2048]
        kcT = [dt(f"kcT{g}", [8, 128, GW[g]], F32, kind="ExternalInput").ap() for g in range(3)]
        vcc = [dt(f"vcc{g}", [GW[g], 8, 128], F32, kind="ExternalInput").ap() for g in range(3)]
        OKR = [128, 512, 1024]
        ok_g = [dt(f"ok_g{g}", [OKR[g], 2, 8, 128], F32, kind="ExternalOutput").ap() for g in range(3)]
        osn = dt("osn", [3, 8, 2, 8, 128], F32, kind="ExternalOutput").ap()
        kt_in = dt("kt_in", [3072, 1024], BF16)
        kt_out = dt("kt_out", [NCORES * 3072, 1024], BF16)
        vt_in = dt("vt_in", [24 * 1024, 128], BF16)
        vt_out = dt("vt_out", [NCORES * 24 * 1024, 128], BF16)
        qt_loc = dt("qt_loc", [3072, NT], BF16)
        tpz = dt("tpz", [24, 128, 400], F32)
        kt_p1 = dt("kt_p1", [3072, 1024], BF16)
        kt_p2 = dt("kt_p2", [1024, 1024], BF16)
        vt_p1 = dt("vt_p1", [24576, 128], BF16)
        vt_p2 = dt("vt_p2", [8192, 128], BF16)
    yT = dt("yT", [D, NT], F32, kind="ExternalOutput").ap()
    o_c0p = dt("o_c0p", [D, 30], F32, kind="ExternalOutput").ap()
    o_c0s = dt("o_c0s", [D, 30], F32, kind="ExternalOutput").ap()
    o_c3p = dt("o_c3p", [D, 30], F32, kind="ExternalOutput").ap()
    o_c3s = dt("o_c3s", [D, 30], F32, kind="ExternalOutput").ap()

    cb_in = [dt(f"cb_in{j}", [D, 32], BF16) for j in range(3)]
    cb_out = [dt(f"cb_out{j}", [9 * D, 32], BF16) for j in range(3)]

    _pid = nc.gpsimd.partition_id()
    _B0 = nc.gpsimd.snap(_pid * (((_pid % 4) + 3) // 4))
    _R1 = nc.gpsimd.snap((_pid + 7) % 8)
    _R2 = nc.gpsimd.snap((_pid + 6) % 8)
    def blk_prev0():
        return _B0

    def wv(w2d):
        return w2d.rearrange("(kc p) n -> p kc n", p=128)

    plan = []
    def plan_linear(w2d, kc0, nkc, ncols_total, col_order=None):
        v = wv(w2d)
        nb = ncols_total // BW
        order = col_order if col_order is not None else list(range(nb))
        for b in order:
            plan.append([(lambda sl, nkc=nkc: sl[:, 0:nkc, :], v[:, kc0:kc0 + nkc, b * BW:(b + 1) * BW])])

    def plan_mlp(i):
        seq = ["a0", "a1", "b0", "a2", "b1", "a3", "b2", "b3"]
        for s in seq:
            q = int(s[1])
            if s[0] == "a":
                plan_linear(ffn_w1[i][:, q * 2048:(q + 1) * 2048], 0, 16, 2048)
            else:
                plan_linear(ffn_w2[i], q * 16, 16, 2048)

    GLU_ORDER = [x for b in range(8) for x in (8 + b, b)]
    for i in range(START_LAYER, NLAYERS):
        kind = i % 3
        if kind == 0:
            j = i // 3
            plan_linear(conv_w_glu[j], 0, 16, 4096, GLU_ORDER)
            plan_linear(conv_w_out[j], 0, 16, 2048)
        if kind == 1:
            plan_linear(lru_w_in, 0, 16, 4096)
            plan.append([(lambda sl: sl[:, :, :], lru_w_a.rearrange("n (k p) c -> p (n k) c", p=128))])
            plan.append([(lambda sl: sl[:, :, :], lru_w_x.rearrange("n (k p) c -> p (n k) c", p=128))])
            plan_linear(lru_w_out, 0, 16, 2048)
        if kind == 2:
            for g in range(3):
                for blk in range(4):
                    c0 = 3072 + g * 1024 + blk * BW
                    plan.append([(lambda sl: sl[:, :, :], wv(att_w_qkv)[:, :, c0:c0 + BW])])
                for blk in range(4):
                    c0 = 6144 + g * 1024 + blk * BW
                    plan.append([(lambda sl: sl[:, :, :], wv(att_w_qkv)[:, :, c0:c0 + BW])])
            plan_linear(att_w_qkv, 0, 16, 3072)
            plan_linear(att_w_o, 0, 8, 2048)
        plan_mlp(i)

    sb = nc.sbuf_tensor
    import contextlib
    with contextlib.ExitStack() as es:
        def SB(name, shape, dtype):
            return es.enter_context(sb(name, shape, dtype))
        X = SB("X", [128, KC, NT], F32)
        PVt = SB("PVt", [128, NPV, KC], F32)
        META = SB("META", [128, 16], F32)
        GS = SB("GS", [128, 9, KC], F32)
        ones_b = SB("ones_b", [128, 128], BF16)
        ident_b = SB("ident_b", [128, 128], BF16)
        zeros_b = SB("zeros_b", [128, KC, 32], BF16)
        WS = [SB(f"wslot{k}", [128, 16, BW], BF16) for k in range(2)]
        UW = 1092
        ARENA = SB("ARENA", [128, KC * NT + KC * UW + KC * 1048], BF16)
        def aview(off, k, w):
            return ARENA[:, off:off + k * w].rearrange("p (k t) -> p k t", k=k)
        RH = aview(0, KC, NT)
        R1 = aview(KC * NT, KC, UW)
        R2 = aview(KC * NT + KC * UW, KC, 1048)
        SQ = [SB(f"sq{k}", [128, 512], BF16) for k in range(3)]
        RSTD = SB("RSTD", [128, 512], F32)
        MEAN = SB("MEAN", [128, 512], F32)
        TMPF = [SB(f"tmpf{k}", [128, 512], F32) for k in range(2)]
        RL = [SB(f"rl{k}", [128, 512], BF16) for k in range(3)]
        EPS_RMS = SB("eps_rms", [128, 1], F32)
        EPS_LN = SB("eps_ln", [128, 1], F32)
        ONE_C = SB("one_c", [128, 1], F32)
        CL = SB("CL", [128, KC], F32)
        CL2 = SB("CL2", [128, KC], F32)
        RS = SB("RS", [128, KC, 2], F32)
        CARB = SB("CARB", [128, 32], F32)
        CAR = SB("CAR", [128, NCORES, 32], F32)
        HIN = SB("HIN", [128, KC], F32)
        LH0 = SB("LH0", [128, KC], F32)
        OLH = SB("OLH", [128, 2, KC], F32)
        TSM = SB("TSM", [128, 8, 8], F32)
        CT = [SB(f"ct{k}", [128, KC], F32) for k in range(2)]
        rl_free = [None] * 3
        rl_i = [0]
        DG = [SB(f"dg{k}", [128, 128], BF16) for k in range(8)]
        PS = [es.enter_context(nc.psum_tensor(f"ps{k}", [128, 512], F32)) for k in range(8)]

        ws = WStream(S, WS, plan)

        t_x = S.dma("sp", X[:], xT.rearrange("(kc p) t -> p kc t", p=128), sem="ld")
        t_pv = S.dma("sp", PVt[:], pvec, sem="ld")
        t_meta = S.dma("sp", META[:], meta, sem="ld")
        t_ld = t_meta
        S.op("pool", lambda e: e.memset(ones_b[:], 1.0))
        S.op("pool", lambda e: e.memset(EPS_RMS[:], RMS_EPS))
        S.op("pool", lambda e: e.memset(EPS_LN[:], LN_EPS))
        S.op("pool", lambda e: e.memset(ONE_C[:], 1.0))
        S.op("pool", lambda e: e.memset(zeros_b[:], 0.0))
        t_i0 = S.op("pool", lambda e: e.memset(ident_b[:], 1.0))
        t_ident = S.op("pool", lambda e: e.affine_select(out=ident_b[:], in_=ident_b[:], pattern=[[-1, 128]],
                                                         compare_op=ALU.is_equal, fill=0.0, base=0,
                                                         channel_multiplier=1), deps=[t_i0])
        for j in range(3):
            S.dma("pool", cb_out[j].ap()[0:D, :].rearrange("(kc p) t -> p kc t", p=128), zeros_b[:],
                  deps=[S.cur("pool")], sem="zb")
        t_gs = S.op("dve", lambda e: e.tensor_scalar(out=GS[:], in0=PVt[:, 0:9, :], scalar1=1.0,
                                                     scalar2=None, op0=ALU.mult), deps=[t_ld])
        S.barrier()

        bank_free = [None] * 8
        sq_free = [None] * 3
        sq_i = [0]

        def colsum_sq(src_fn, tt, bank, scale_sq=True):
            t0, n = TT[tt]
            last = None
            for kc in range(KC):
                k = sq_i[0] % 3; sq_i[0] += 1
                tq = S.op("act", lambda e, kc=kc, k=k: e.activation(out=SQ[k][:, 0:n], in_=src_fn(kc, t0, n), func=AF.Square),
                          deps=[sq_free[k]])
                deps = [tq] + ([bank_free[bank]] if kc == 0 else [])
                last = S.op("pe", lambda e, kc=kc, k=k: e.matmul(PS[bank][:, 0:n], lhsT=ones_b[:], rhs=SQ[k][:, 0:n],
                                                                 start=(kc == 0), stop=(kc == KC - 1)), deps=deps)
                sq_free[k] = last
            return last

        def colsum(src_fn, tt, bank):
            t0, n = TT[tt]
            last = None
            for kc in range(KC):
                deps = [bank_free[bank]] if kc == 0 else []
                last = S.op("pe", lambda e, kc=kc: e.matmul(PS[bank][:, 0:n], lhsT=ones_b[:], rhs=src_fn(kc, t0, n),
                                                            start=(kc == 0), stop=(kc == KC - 1)), deps=deps,
                            mark=(kc == KC - 1))
            return last

        def rmsnorm(gidx, x_ready, dst):
            toks = []
            for tt in range(3):
                t0, n = TT[tt]
                bank = 6 + (tt % 2)
                S.wait("act", x_ready)
                tp = colsum_sq(lambda kc, t0, n: X[:, kc, t0:t0 + n], tt, bank)
                tsq = S.op("act", lambda e: e.activation(out=RSTD[:, 0:n], in_=PS[bank][:, 0:n], func=AF.Sqrt,
                                                         bias=EPS_RMS[:, 0:1], scale=1.0 / D), deps=[tp, S.cur("dve")])
                tr = S.op("dve", lambda e: e.reciprocal(out=RSTD[:, 0:n], in_=RSTD[:, 0:n]), deps=[tsq])
                bank_free[bank] = tsq
                for kc in range(KC):
                    tk = S.op("dve", lambda e, kc=kc: e.scalar_tensor_tensor(
                        out=dst[:, kc, t0:t0 + n], in0=X[:, kc, t0:t0 + n], scalar=GS[:, gidx, kc:kc + 1],
                        in1=RSTD[:, 0:n], op0=ALU.mult, op1=ALU.mult), deps=[tr, x_ready])
                toks.append(tk)
            return toks

        grp = [0]
        def linear(nblocks, nkc, rhs_fn, evac_fn, rhs_ready, chunk_of):
            S.wait("pe", rhs_ready)
            for b in range(nblocks):
                slot, ltok = ws.acquire()
                S.wait("pe", ltok)
                last = None
                for mi in range(CPB):
                    m = chunk_of(b, mi)
                    g = grp[0] % 2; grp[0] += 1
                    banks = [3 * g, 3 * g + 1, 3 * g + 2]
                    for bk in banks: S.wait("pe", bank_free[bk])
                    for kc in range(nkc):
                        for tt in range(3):
                            t0, n = TT[tt]
                            ins = nc.tensor.matmul(PS[banks[tt]][:, 0:n], lhsT=slot[:, kc, mi * 128:(mi + 1) * 128],
                                                   rhs=rhs_fn(kc, t0, n), start=(kc == 0), stop=(kc == nkc - 1))
                    last = S.mark("pe", ins)
                    for tt in range(3):
                        t0, n = TT[tt]
                        bank_free[banks[tt]] = evac_fn(m, tt, PS[banks[tt]][:, 0:n], last)
                ws.release(last)

        def mlp(i, x_ready):
            th = rmsnorm(4 + i, x_ready, RH)
            hid = [R1, R2]
            hid_ready = {}
            hid_free = [S.cur("pe"), S.cur("pe")]
            xr = [x_ready]

            def evac_h(q):
                def f(m, tt, ps, tok):
                    t0, n = TT[tt]
                    k = rl_i[0] % 3; rl_i[0] += 1
                    ta = S.op("act", lambda e: e.activation(out=RL[k][:, 0:n], in_=ps, func=AF.Relu), deps=[tok, rl_free[k]])
                    td = S.op("dve", lambda e: e.tensor_tensor(out=hid[q % 2][:, m, t0:t0 + n], in0=ps, in1=RL[k][:, 0:n],
                                                               op=ALU.mult), deps=[ta, hid_free[q % 2]])
                    rl_free[k] = td
                    return td
                return f

            def evac_x(m, tt, ps, tok):
                t0, n = TT[tt]
                t = S.op("dve", lambda e: e.tensor_tensor(out=X[:, m, t0:t0 + n], in0=X[:, m, t0:t0 + n], in1=ps,
                                                          op=ALU.add), deps=[tok])
                xr[0] = t
                return t

            for s in ["a0", "a1", "b0", "a2", "b1", "a3", "b2", "b3"]:
                q = int(s[1])
                if s[0] == "a":
                    linear(2048 // BW, 16, lambda kc, t0, n: RH[:, kc, t0:t0 + n], evac_h(q), th, lambda b, mi: b * CPB + mi)
                    hid_ready[q] = S.cur("dve")
                else:
                    linear(2048 // BW, 16, lambda kc, t0, n, q=q: hid[q % 2][:, kc, t0:t0 + n], evac_x, [hid_ready[q]],
                           lambda b, mi: b * CPB + mi)
                    hid_free[q % 2] = S.cur("pe")
            return S.cur("dve")

        def conv_layer(i, j, x_ready, scT, o_p, o_s, cbi):
            U = R1
            C = RH
            th = rmsnorm(i, x_ready, RH)
            t_st = S.dma("pool", U[:, :, 1054:1084], scT.rearrange("(kc p) t -> p kc t", p=128), sem="cst")

            def ucol(t0, n):
                return (30 + t0) if t0 < NPR else (1084 + (t0 - NPR))

            def evac_glu(m, tt, ps, tok):
                t0, n = TT[tt]
                c0 = ucol(t0, n)
                if m >= 16:
                    return S.op("act", lambda e: e.activation(out=U[:, m - 16, c0:c0 + n], in_=ps, func=AF.Sigmoid), deps=[tok])
                return S.op("dve", lambda e: e.tensor_tensor(out=U[:, m, c0:c0 + n], in0=ps, in1=U[:, m, c0:c0 + n],
                                                             op=ALU.mult), deps=[tok, S.cur("act")])

            def glu_chunk(b, mi):
                blk = GLU_ORDER[b]
                return blk * CPB + mi
            linear(4096 // BW, 16, lambda kc, t0, n: RH[:, kc, t0:t0 + n], evac_glu, th, glu_chunk)
            t_u = S.cur("dve")
            cin = cb_in[cbi].ap().rearrange("(kc p) t -> p kc t", p=128)
            t_b = S.dma("pool", cin[:, :, 0:30], U[:, :, 30 + NPR - 30:30 + NPR], deps=[t_u], sem="cb")
            S.wait("pool", t_b)
            S.wait("pool", ("d", "zb", S.dcnt["zb"]))
            t_ag = S.op("pool", lambda e: e.collective_compute(
                "AllGather", ALU.bypass, replica_groups=[list(range(NCORES))],
                ins=[cb_in[cbi].ap().opt()], outs=[cb_out[cbi].ap()[D:9 * D, :].opt()]))
            S.wait("pool", t_ag)
            src = cb_out[cbi].ap()[bass.ds(blk_prev0() * D, D), :].rearrange("(kc p) t -> p kc t", p=128)
            t_halo = S.dma("pool", U[:, :, 0:30], src[:, :, 0:30], sem="cb")
            t_o1 = S.dma("pool", o_p.rearrange("(kc p) t -> p kc t", p=128), U[:, :, 30 + NPR - 30:30 + NPR], deps=[t_u], sem="out")
            t_o2 = S.dma("pool", o_s.rearrange("(kc p) t -> p kc t", p=128), U[:, :, 1062:1092], deps=[t_u, t_st], sem="out")

            dg_free = [None] * 8
            dgi = 0
            conv_done = None
            for m in range(KC):
                g = grp[0] % 2; grp[0] += 1
                banks = [3 * g, 3 * g + 1, 3 * g + 2]
                for tap in range(CW):
                    k = dgi % 8; dgi += 1
                    wcol = PVt[:, PV[f"cdw{j}_{tap}"], m:m + 1]
                    if dgi % 2 == 0:
                        td = S.op("act", lambda e, k=k, wcol=wcol: e.activation(out=DG[k][:], in_=ident_b[:], func=AF.Identity, scale=wcol),
                                  deps=[dg_free[k]])
                    else:
                        td = S.op("dve", lambda e, k=k, wcol=wcol: e.tensor_scalar(out=DG[k][:], in0=ident_b[:], scalar1=wcol,
                                                                                  scalar2=None, op0=ALU.mult), deps=[dg_free[k]])
                    S.wait("pe", td)
                    if tap == 0:
                        for bk in banks: S.wait("pe", bank_free[bk])
                        S.wait("pe", t_u); S.wait("pe", t_halo); S.wait("pe", t_st)
                    for tt in range(3):
                        t0, n = TT[tt]
                        c0 = (t0 + tap) if t0 < NPR else (1054 + tap)
                        ins = nc.tensor.matmul(PS[banks[tt]][:, 0:n], lhsT=DG[k][:], rhs=U[:, m, c0:c0 + n],
                                               start=(tap == 0), stop=(tap == CW - 1))
                    dg_free[k] = S.mark("pe", ins)
                tok = dg_free[k]
                for tt in range(3):
                    t0, n = TT[tt]
                    conv_done = S.op("act", lambda e, tt=tt, t0=t0, n=n: e.activation(
                        out=C[:, m, t0:t0 + n], in_=PS[banks[tt]][:, 0:n], func=AF.Identity,
                        bias=PVt[:, PV[f"cdb{j}"], m:m + 1], scale=1.0), deps=[tok])
                    bank_free[banks[tt]] = conv_done
            tln = []
            for tt in range(3):
                t0, n = TT[tt]
                S.wait("pe", conv_done)
                t1 = colsum(lambda kc, t0, n: C[:, kc, t0:t0 + n], tt, 6)
                tm = S.op("dve", lambda e: e.tensor_scalar(out=MEAN[:, 0:n], in0=PS[6][:, 0:n], scalar1=1.0 / D,
                                                           scalar2=None, op0=ALU.mult), deps=[t1])
                bank_free[6] = tm
                S.wait("act", conv_done)
                t2 = colsum_sq(lambda kc, t0, n: C[:, kc, t0:t0 + n], tt, 7)
                ta = S.op("dve", lambda e: e.tensor_tensor(out=TMPF[0][:, 0:n], in0=MEAN[:, 0:n], in1=MEAN[:, 0:n],
                                                           op=ALU.mult), deps=[tm, S.cur("act")])
                tb = S.op("dve", lambda e: e.scalar_tensor_tensor(out=TMPF[1][:, 0:n], in0=PS[7][:, 0:n], scalar=1.0 / D,
                                                                  in1=TMPF[0][:, 0:n], op0=ALU.mult, op1=ALU.subtract),
                          deps=[t2, ta])
                bank_free[7] = tb
                tsq = S.op("act", lambda e: e.activation(out=RSTD[:, 0:n], in_=TMPF[1][:, 0:n], func=AF.Sqrt,
                                                         bias=EPS_LN[:, 0:1], scale=1.0), deps=[tb])
                tr = S.op("dve", lambda e: e.reciprocal(out=RSTD[:, 0:n], in_=RSTD[:, 0:n]), deps=[tsq])
                for kc in range(KC):
                    k = kc % 2
                    tc1 = S.op("dve", lambda e, kc=kc, k=k: e.tensor_tensor(out=TMPF[k][:, 0:n], in0=C[:, kc, t0:t0 + n],
                                                                            in1=MEAN[:, 0:n], op=ALU.subtract),
                               deps=[tr, S.cur("act")])
                    tc2 = S.op("dve", lambda e, kc=kc, k=k: e.tensor_tensor(out=TMPF[k][:, 0:n], in0=TMPF[k][:, 0:n],
                                                                            in1=RSTD[:, 0:n], op=ALU.mult), deps=[tc1])
                    tc3 = S.op("act", lambda e, kc=kc, k=k: e.activation(
                        out=C[:, kc, t0:t0 + n], in_=TMPF[k][:, 0:n], func=AF.Silu,
                        bias=PVt[:, PV[f"clb{j}"], kc:kc + 1], scale=PVt[:, PV[f"clg{j}"], kc:kc + 1]), deps=[tc2])
                tln.append(tc3)
            xr = [x_ready]
            def evac_x(m, tt, ps, tok):
                t0, n = TT[tt]
                t = S.op("dve", lambda e: e.tensor_tensor(out=X[:, m, t0:t0 + n], in0=X[:, m, t0:t0 + n], in1=ps, op=ALU.add),
                         deps=[tok])
                return t
            linear(2048 // BW, 16, lambda kc, t0, n: C[:, kc, t0:t0 + n], evac_x, tln, lambda b, mi: b * CPB + mi)
            return S.cur("dve")


        def lru_layer(i, x_ready):
            Y = R1
            XBe = R2
            XC = RH
            R2OFF = KC * NT + KC * UW
            def TV(k):
                return ARENA[:, R2OFF + k * 1024:R2OFF + (k + 1) * 1024].bitcast(F32)
            th = rmsnorm(i, x_ready, RH)
            t_st = S.dma("pool", XBe[:, :, 1027:1030], lcsT.rearrange("(kc p) t -> p kc t", p=128), sem="cst")
            t_h0 = S.dma("sp", LH0[:], lh0, sem="ld")
            lam = PVt[:, PV["llam"], :]
            t1 = S.op("act", lambda e: e.activation(out=CT[0][:], in_=lam, func=AF.Exp, scale=-1.0))
            t2 = S.op("act", lambda e: e.activation(out=CT[0][:], in_=CT[0][:], func=AF.Ln, bias=ONE_C[:, 0:1], scale=1.0), deps=[t1])
            t3 = S.op("dve", lambda e: e.tensor_scalar(out=CL[:], in0=CT[0][:], scalar1=-8.0, scalar2=None, op0=ALU.mult), deps=[t2])
            t_cl = S.op("dve", lambda e: e.tensor_scalar(out=CL2[:], in0=CT[0][:], scalar1=-16.0, scalar2=None, op0=ALU.mult), deps=[t2])

            def xcol(t0):
                return 3 + t0 if t0 < NPR else 1030 + (t0 - NPR)

            def evac_in(m, tt, ps, tok):
                t0, n = TT[tt]
                if m < 16:
                    return S.op("act", lambda e: e.activation(out=Y[:, m, t0:t0 + n], in_=ps, func=AF.Gelu_apprx_tanh), deps=[tok])
                c0 = xcol(t0)
                return S.op("dve", lambda e: e.tensor_copy(out=XBe[:, m - 16, c0:c0 + n], in_=ps), deps=[tok])
            linear(4096 // BW, 16, lambda kc, t0, n: RH[:, kc, t0:t0 + n], evac_in, th, lambda b, mi: b * CPB + mi)
            t_in = [S.cur("act"), S.cur("dve"), S.cur("pe")]
            cin = cb_in[2].ap().rearrange("(kc p) t -> p kc t", p=128)
            t_b = S.dma("pool", cin[:, :, 0:3], XBe[:, :, 1024:1027], deps=t_in, sem="cb")
            S.wait("pool", t_b)
            t_ag = S.op("pool", lambda e: e.collective_compute(
                "AllGather", ALU.bypass, replica_groups=[list(range(NCORES))],
                ins=[cb_in[2].ap().opt()], outs=[cb_out[2].ap()[D:9 * D, :].opt()]))
            S.wait("pool", t_ag)
            src = cb_out[2].ap()[bass.ds(blk_prev0() * D, D), :].rearrange("(kc p) t -> p kc t", p=128)
            t_halo = S.dma("pool", XBe[:, :, 0:3], src[:, :, 0:3], sem="cb")
            t_o1 = S.dma("pool", o_lcp.rearrange("(kc p) t -> p kc t", p=128), XBe[:, :, 1024:1027], deps=t_in, sem="out")
            t_o2 = S.dma("pool", o_lcs.rearrange("(kc p) t -> p kc t", p=128), XBe[:, :, 1035:1038], deps=t_in, sem="out")
            wj = [lambda m, j=j: PVt[:, PV[f"lcw{j}"], m:m + 1] for j in range(4)]
            S.wait("dve", [t_halo, t_st] + t_in)
            for m in range(KC):
                for (tts, tmps) in (((0, 1), (TMPF[0], TMPF[1])), ((2,), (TMPF[0],))):
                    for j in range(4):
                        for tt, TF in zip(tts, tmps):
                            t0, n = TT[tt]
                            c0 = (t0 if t0 < NPR else 1027 + (t0 - NPR)) + j
                            src_ap = XBe[:, m, c0:c0 + n]
                            if j == 0:
                                S.op("dve", lambda e, src_ap=src_ap, TF=TF, n=n: e.tensor_scalar(
                                    out=TF[:, 0:n], in0=src_ap, scalar1=wj[0](m), scalar2=PVt[:, PV["lcb"], m:m + 1],
                                    op0=ALU.mult, op1=ALU.add), deps=[S.cur("dve")])
                            elif j < 3:
                                S.op("dve", lambda e, src_ap=src_ap, TF=TF, n=n, j=j: e.scalar_tensor_tensor(
                                    out=TF[:, 0:n], in0=src_ap, scalar=wj[j](m), in1=TF[:, 0:n], op0=ALU.mult, op1=ALU.add),
                                    deps=[S.cur("dve")])
                            else:
                                S.op("dve", lambda e, src_ap=src_ap, TF=TF, n=n, t0=t0: e.scalar_tensor_tensor(
                                    out=XC[:, m, t0:t0 + n], in0=src_ap, scalar=wj[3](m), in1=TF[:, 0:n], op0=ALU.mult, op1=ALU.add),
                                    deps=[S.cur("dve")])
            t_xc = S.cur("dve")
            S.wait("dve", [t_o1, t_o2, t_b])

            slotA, tokA = ws.peek(0)
            slotX, tokX = ws.peek(1)
            S.wait("pe", [tokA, tokX, t_xc])
            set_free = [None, None]

            def gate_pass(passB):
                tiles = [0, 1, 2] if passB else [0, 1]
                for m in range(KC):
                    n_, e_ = divmod(m, 2)
                    gtok = {}
                    for which, slot, banks in (("a", slotA, [0, 1, 2]), ("x", slotX, [3, 4, 5])):
                        for bk in banks: S.wait("pe", bank_free[bk])
                        for tt in tiles:
                            t0, n = TT[tt]
                            for k2 in range(2):
                                ins = nc.tensor.matmul(PS[banks[tt]][:, 0:n], lhsT=slot[:, 2 * n_ + k2, e_ * 128:(e_ + 1) * 128],
                                                       rhs=XC[:, 2 * n_ + k2, t0:t0 + n], start=(k2 == 0), stop=(k2 == 1))
                        gtok[which] = S.mark("pe", ins)
                    st_ = m % 2
                    def T(tt, k):
                        if tt < 2:
                            return TV(st_ * 8 + tt * 4 + k)
                        return TSM[:, st_ * 4 + k, :]
                    sf = set_free[st_]
                    tR, tI, tA, tS = {}, {}, {}, {}
                    for tt in tiles:
                        t0, n = TT[tt]
                        kw = {}
                        if not passB:
                            kw["accum_out"] = RS[:, m, tt:tt + 1]
                        tR[tt] = S.op("act", lambda e, tt=tt, n=n, kw=kw: e.activation(
                            out=T(tt, 0)[:, 0:n], in_=PS[tt][:, 0:n], func=AF.Sigmoid, bias=PVt[:, PV["lba"], m:m + 1], scale=1.0, **kw),
                            deps=[gtok["a"], sf])
                        bank_free[tt] = tR[tt]
                    for tt in tiles:
                        t0, n = TT[tt]
                        tI[tt] = S.op("act", lambda e, tt=tt, n=n: e.activation(
                            out=T(tt, 1)[:, 0:n], in_=PS[3 + tt][:, 0:n], func=AF.Sigmoid, bias=PVt[:, PV["lbx"], m:m + 1], scale=1.0),
                            deps=[gtok["x"], sf])
                        bank_free[3 + tt] = tI[tt]
                    for tt in tiles:
                        t0, n = TT[tt]
                        tA[tt] = S.op("act", lambda e, tt=tt, n=n: e.activation(
                            out=T(tt, 2)[:, 0:n], in_=T(tt, 0)[:, 0:n], func=AF.Exp, scale=CL[:, m:m + 1]), deps=[tR[tt]])
                        S.op("act", lambda e, tt=tt, n=n: e.activation(
                            out=T(tt, 0)[:, 0:n], in_=T(tt, 0)[:, 0:n], func=AF.Exp, scale=CL2[:, m:m + 1]))
                    t_e2 = S.cur("act")
                    for tt in tiles:
                        t0, n = TT[tt]
                        tS[tt] = S.op("act", lambda e, tt=tt, n=n: e.activation(
                            out=T(tt, 0)[:, 0:n], in_=T(tt, 0)[:, 0:n], func=AF.Sqrt, bias=ONE_C[:, 0:1], scale=-1.0), deps=[t_e2])
                    tIX = {}
                    for tt in tiles:
                        t0, n = TT[tt]
                        tIX[tt] = S.op("dve", lambda e, tt=tt, n=n, t0=t0: e.tensor_tensor(
                            out=T(tt, 1)[:, 0:n], in0=T(tt, 1)[:, 0:n], in1=XC[:, m, t0:t0 + n], op=ALU.mult), deps=[tI[tt]])
                    tB = {}
                    for tt in tiles:
                        t0, n = TT[tt]
                        tB[tt] = S.op("dve", lambda e, tt=tt, n=n: e.tensor_tensor(
                            out=T(tt, 1)[:, 0:n], in0=T(tt, 1)[:, 0:n], in1=T(tt, 0)[:, 0:n], op=ALU.mult), deps=[tS[tt], tIX[tt]])
                    tsc = {}
                    for tt in tiles:
                        t0, n = TT[tt]
                        if tt == 0:
                            init = HIN[:, m:m + 1] if passB else 0.0
                            dps = [tA[tt], tB[tt]]
                        elif tt == 1:
                            init = T(0, 3)[:, 511:512]
                            dps = [tA[tt], tB[tt], tsc[0]]
                        else:
                            init = LH0[:, m:m + 1]
                            dps = [tA[tt], tB[tt], t_h0]
                        tsc[tt] = S.op("dve", lambda e, tt=tt, n=n, init=init: e.tensor_tensor_scan(
                            out=T(tt, 3)[:, 0:n], data0=T(tt, 2)[:, 0:n], data1=T(tt, 1)[:, 0:n], initial=init,
                            op0=ALU.mult, op1=ALU.add), deps=dps)
                    if not passB:
                        tl = S.op("dve", lambda e: e.tensor_copy(out=CARB[:, 16 + m:17 + m], in_=T(1, 3)[:, 511:512]), deps=[tsc[1]])
                    else:
                        S.op("dve", lambda e: e.tensor_copy(out=OLH[:, 0, m:m + 1], in_=T(1, 3)[:, 511:512]), deps=[tsc[1]])
                        S.op("dve", lambda e: e.tensor_copy(out=OLH[:, 1, m:m + 1], in_=T(2, 3)[:, 7:8]), deps=[tsc[2]])
                        for tt in tiles:
                            t0, n = TT[tt]
                            tl = S.op("dve", lambda e, tt=tt, n=n, t0=t0: e.tensor_tensor(
                                out=Y[:, m, t0:t0 + n], in0=T(tt, 3)[:, 0:n], in1=Y[:, m, t0:t0 + n], op=ALU.mult), deps=[tsc[tt]])
                    set_free[st_] = S.cur("dve")

            gate_pass(False)
            ta1 = S.op("dve", lambda e: e.tensor_tensor(out=CT[0][:], in0=RS[:, :, 0], in1=RS[:, :, 1], op=ALU.add),
                       deps=[S.cur("act"), S.cur("dve")])
            ta2 = S.op("dve", lambda e: e.tensor_tensor(out=CT[0][:], in0=CT[0][:], in1=CL[:], op=ALU.mult), deps=[ta1])
            ta3 = S.op("act", lambda e: e.activation(out=CARB[:, 0:16], in_=CT[0][:], func=AF.Exp), deps=[ta2])
            t_cb = S.dma("pool", lc_in.ap(), CARB[:], deps=[ta3, S.cur("dve")], sem="cb")
            S.wait("pool", t_cb)
            t_ag2 = S.op("pool", lambda e: e.collective_compute(
                "AllGather", ALU.bypass, replica_groups=[list(range(NCORES))],
                ins=[lc_in.ap().opt()], outs=[lc_out.ap().opt()]))
            S.wait("pool", t_ag2)
            t_car = S.dma("pool", CAR[:], lc_out.ap().rearrange("(r p) c -> p r c", p=128), sem="cb")
            tz = S.op("dve", lambda e: e.memset(HIN[:], 0.0), deps=[t_car])
            for r in range(NCORES):
                A_r = CAR[:, r, 0:16]; H_r = CAR[:, r, 16:32]
                tq = S.op("dve", lambda e, A_r=A_r: e.tensor_tensor(out=CT[1][:], in0=A_r, in1=HIN[:], op=ALU.mult), deps=[S.cur("dve")])
                tq = S.op("dve", lambda e, H_r=H_r: e.tensor_tensor(out=CT[1][:], in0=CT[1][:], in1=H_r, op=ALU.add), deps=[tq])
                tq = S.op("dve", lambda e: e.tensor_tensor(out=CT[1][:], in0=CT[1][:], in1=HIN[:], op=ALU.subtract), deps=[tq])
                tq = S.op("dve", lambda e, r=r: e.scalar_tensor_tensor(out=HIN[:], in0=CT[1][:], scalar=META[:, 4 + r:5 + r],
                                                                     in1=HIN[:], op0=ALU.mult, op1=ALU.add), deps=[tq])
            gate_pass(True)
            t_g = S.cur("dve")
            S.dma("sp", o_lhp, OLH[:, 0, :], deps=[t_g], sem="out")
            S.dma("sp", o_lhs, OLH[:, 1, :], deps=[t_g], sem="out")
            pe_done = S.cur("pe")
            ws.release(pe_done)
            ws.release(pe_done)

            def evac_x(m, tt, ps, tok):
                t0, n = TT[tt]
                return S.op("dve", lambda e: e.tensor_tensor(out=X[:, m, t0:t0 + n], in0=X[:, m, t0:t0 + n], in1=ps, op=ALU.add),
                            deps=[tok])
            linear(2048 // BW, 16, lambda kc, t0, n: Y[:, kc, t0:t0 + n], evac_x, [t_g], lambda b, mi: b * CPB + mi)
            return S.cur("dve")


        def att_layer(i, x_ready):
            SCALE = 128 ** -0.5
            AOFF1 = KC * NT
            def AV(off, n):
                return ARENA[:, off:off + n]
            def AVF(off, n):
                return ARENA[:, off:off + 2 * n].bitcast(F32)
            H = RH
            o = AOFF1
            KST = [AV(o + k * NT, NT) for k in range(2)]; o += 2 * NT
            QST = [AV(o + k * NT, NT) for k in range(2)]; o += 2 * NT
            VST = [AV(o + k * 256, 256) for k in range(2)]; o += 512
            VSF = [AVF(o + k * 512, 256) for k in range(2)]; o += 1024
            SEL = AVF(o, 1200).rearrange("p (g c) -> p g c", g=3); o += 2400
            TPS = [AVF(o + k * 800, 400) for k in range(2)]; o += 1600
            LT = [AVF(o + k * 256, 128) for k in range(2)]; o += 512
            RB = AVF(o, 24); o += 48
            ONE33 = AVF(o, 128); o += 256
            assert o <= AOFF1 + KC * UW
            AEND = KC * NT + KC * UW + KC * 1048
            VNS = ARENA[:, AEND - 3072:AEND].rearrange("p (g h d) -> p g h d", g=3, h=8)
            KNS = ARENA[:, AEND - 3072 - 192:AEND - 3072].rearrange("p (g h t) -> p g h t", g=3, h=8)

            th = rmsnorm(i, x_ready, RH)
            if ATT_STAGE <= 0:
                S.barrier()
                return S.cur("dve")
            t_sel = S.dma("sp", SEL[0:33], selc, sem="sel")
            t_m1 = S.op("dve", lambda e: e.memset(RB[0:33], -30000.0))
            t_m2 = S.op("dve", lambda e: e.memset(ONE33[0:33], 1.0))
            t_rb = S.dma("sp", RB[0:32], rel_bias, deps=[t_m1], sem="rb")
            tps_free = [None, None]; lt_free = [None, None]
            t_tp = None
            for col in range(0 if 'T' not in ATT_SKIP else 24, 24):
                k = col % 2
                tl = S.op("dve", lambda e, k=k, col=col: e.tensor_scalar(out=LT[k][0:33], in0=ONE33[0:33], scalar1=RB[0:33, col:col + 1],
                                                                        scalar2=None, op0=ALU.mult), deps=[t_rb, t_m2, lt_free[k]])
                tm = S.op("pe", lambda e, k=k, col=col: e.matmul(PS[6 + k][:, 0:400], lhsT=LT[k][0:33], rhs=SEL[0:33, col // 8, :],
                                                                 start=True, stop=True), deps=[tl, t_sel, bank_free[6 + k]])
                lt_free[k] = tm
                te = S.op("act", lambda e, k=k: e.activation(out=TPS[k][:], in_=PS[6 + k][:, 0:400], func=AF.Copy), deps=[tm, tps_free[k]])
                bank_free[6 + k] = te
                t_tp = S.dma("sp", tpz.ap()[col], TPS[k][:], deps=[te], sem=f"tpz{k}")
                tps_free[k] = t_tp

            S.wait("pe", th)
            kst_free = [None, None]; vst_free = [None, None]; vsf_free = [None, None]
            cnt = {"k": 0, "v": 0, "f": 0, "ps": 0}
            kt_in3 = kt_in.ap()
            vt_in4 = vt_in.ap().rearrange("(g h r) d -> g h r d", g=3, h=8)

            def tokmajor(slot, sets, handler):
                last = None
                for (c0, cstep, M, info) in sets:
                    bk = 3 + (cnt["ps"] % 3); cnt["ps"] += 1
                    S.wait("pe", bank_free[bk])
                    for kc in range(KC):
                        ins = nc.tensor.matmul(PS[bk][0:M, 0:256], lhsT=H[:, kc, bass.ds(c0, M, cstep)] if cstep != 1 else H[:, kc, c0:c0 + M],
                                               rhs=slot[:, kc, :], start=(kc == 0), stop=(kc == KC - 1))
                    last = S.mark("pe", ins)
                    bank_free[bk] = handler(PS[bk], M, info, last)
                return last

            def out_rows_f32(ps, M, tok, dst_ap):
                k = cnt["f"] % 2; cnt["f"] += 1
                tc_ = S.op("act", lambda e: e.activation(out=VSF[k][0:M, :], in_=ps[0:M, 0:256], func=AF.Copy), deps=[tok, vsf_free[k]])
                if 'o' not in ATT_SKIP:
                    vsf_free[k] = S.dma("sp", dst_ap, VSF[k][0:M, :].rearrange("p (h d) -> p h d", h=2), deps=[tc_], sem=f"okv{k}")
                return tc_

            for g in range(3):
                dil = [1, 4, 16][g]
                for blk in range(4):
                    slot, ltok = ws.acquire()
                    S.wait("pe", ltok)
                    for mi in range(2):
                        hh = blk * 2 + mi
                        for bk in (0, 1, 2): S.wait("pe", bank_free[bk])
                        for kc in range(KC):
                            for tt in range(3):
                                t0, n = TT[tt]
                                ins = nc.tensor.matmul(PS[tt][:, 0:n], lhsT=slot[:, kc, mi * 128:(mi + 1) * 128], rhs=H[:, kc, t0:t0 + n],
                                                       start=(kc == 0), stop=(kc == KC - 1))
                        tk = S.mark("pe", ins)
                        k = cnt["k"] % 2; cnt["k"] += 1
                        te = None
                        for tt in range(3):
                            t0, n = TT[tt]
                            eng = "act" if tt != 1 else "dve"
                            if eng == "act":
                                te = S.op("act", lambda e, tt=tt, t0=t0, n=n: e.activation(out=KST[k][:, t0:t0 + n], in_=PS[tt][:, 0:n], func=AF.Copy),
                                          deps=[tk, kst_free[k]])
                            else:
                                te = S.op("dve", lambda e, tt=tt, t0=t0, n=n: e.tensor_copy(out=KST[k][:, t0:t0 + n], in_=PS[tt][:, 0:n]),
                                          deps=[tk, kst_free[k]])
                            bank_free[tt] = te
                        tdeps = [S.cur("act"), S.cur("dve")]
                        r0 = (g * 8 + hh) * 128
                        t_d1 = S.dma("sp", kt_in3[r0:r0 + 128, :], KST[k][:, 0:NPR], deps=tdeps, sem=f"kst{k}") if 'q' not in ATT_SKIP else None
                        t_cp = S.op("dve", lambda e, hh=hh: e.tensor_copy(out=KNS[:, g, hh, :], in_=KST[k][:, NPR:NT]), deps=tdeps)
                        kst_free[k] = [t_d1, t_cp]
                    nb0 = [7, 4, 0][g]
                    sets = [(b * 128, 1, 128, ("p", b - nb0)) for b in range(nb0, 8)] + [(NPR, 1, 8, ("s", 0))]
                    def hk(ps, M, info, tok, blk=blk, g=g):
                        if info[0] == "p":
                            dst = ok_g[g][info[1] * 128:info[1] * 128 + 128, 0, blk * 2:blk * 2 + 2, :]
                        else:
                            dst = osn[g, :, 0, blk * 2:blk * 2 + 2, :]
                        return out_rows_f32(ps, M, tok, dst)
                    if 'k' not in ATT_SKIP:
                        last = tokmajor(slot, sets, hk)
                    else:
                        last = S.cur("pe")
                    ws.release(last)
                for blk in range(4):
                    slot, ltok = ws.acquire()
                    S.wait("pe", ltok)
                    if g == 0:
                        sets = [(b * 128, 1, 128, (b * 128, b * 128, 1, b == 7)) for b in range(8)]
                    elif g == 1:
                        sets = [(r + 512 * n, 4, 128, (r * 256 + n * 128, r, 4, n == 1)) for r in range(4) for n in range(2)]
                    else:
                        sets = [(r, 16, 64, (r * 64, r, 16, True)) for r in range(16)]
                    sets = sets + [(NPR, 1, 8, None)]
                    def hv(ps, M, info, tok, blk=blk, g=g):
                        k = cnt["v"] % 2; cnt["v"] += 1
                        if info is None:
                            veng = "act"
                            if veng == "act":
                                tc_ = S.op("act", lambda e: e.activation(out=VNS[0:8, g, blk * 2:blk * 2 + 2, :],
                                                                         in_=ps[0:8, 0:256].rearrange("p (h d) -> p h d", h=2), func=AF.Copy), deps=[tok])
                            else:
                                tc_ = S.op("dve", lambda e: e.tensor_copy(out=VNS[0:8, g, blk * 2:blk * 2 + 2, :],
                                                                          in_=ps[0:8, 0:256].rearrange("p (h d) -> p h d", h=2)), deps=[tok])
                            out_rows_f32(ps, M, tok, osn[g, :, 1, blk * 2:blk * 2 + 2, :])
                            return [tc_, S.cur("act")]
                        srow0, row0, rstep, want_out = info
                        if True:
                            tc_ = S.op("act", lambda e: e.activation(out=VST[k][0:M, :], in_=ps[0:M, 0:256], func=AF.Copy), deps=[tok, vst_free[k]])
                        else:
                            tc_ = S.op("dve", lambda e: e.tensor_copy(out=VST[k][0:M, :], in_=ps[0:M, 0:256]), deps=[tok, vst_free[k]])
                        dst = vt_in4[g, blk * 2:blk * 2 + 2, srow0:srow0 + M, :].rearrange("h p d -> p h d")
                        if 'd' not in ATT_SKIP:
                            vst_free[k] = S.dma("sp", dst, VST[k][0:M, :].rearrange("p (h d) -> p h d", h=2), deps=[tc_], sem=f"vst{k}")
                        toks = [tc_]
                        if want_out:
                            if g == 0:
                                dsto = ok_g[0][:, 1, blk * 2:blk * 2 + 2, :]
                            elif g == 1:
                                dsto = ok_g[1][bass.ds(row0, M, rstep), 1, blk * 2:blk * 2 + 2, :]
                            else:
                                dsto = ok_g[2][bass.ds(row0, M, rstep), 1, blk * 2:blk * 2 + 2, :]
                            toks.append(out_rows_f32(ps, M, tok, dsto))
                        return toks
                    if 'V' not in ATT_SKIP:
                        last = tokmajor(slot, sets, hv)
                    else:
                        last = S.cur("pe")
                    ws.release(last)
            if ATT_STAGE <= 1:
                for nm in ("kst0", "kst1", "vst0", "vst1", "okv0", "okv1"):
                    if nm not in S.dsem:
                        S.dsem[nm] = nc.alloc_semaphore(f"ds_{nm}"); S.dcnt[nm] = 0
                S.barrier()
                return S.cur("dve")
            S.wait("pool", [("d", nm, S.dcnt[nm]) for nm in ("kst0", "kst1", "vst0", "vst1")])
            t_agk = S.op("pool", lambda e: e.collective_compute("AllGather", ALU.bypass, replica_groups=[list(range(NCORES))],
                                                                ins=[kt_in.ap().opt()], outs=[kt_out.ap().opt()]))
            t_agv = S.op("pool", lambda e: e.collective_compute("AllGather", ALU.bypass, replica_groups=[list(range(NCORES))],
                                                                ins=[vt_in.ap().opt()], outs=[vt_out.ap().opt()]))
            qst_free = [None, None]
            qcnt = [0]
            def evac_q(m, tt, ps, tok):
                t0, n = TT[tt]
                k = (qcnt[0] // 3) % 2; qcnt[0] += 1
                if tt != 1:
                    te = S.op("act", lambda e: e.activation(out=QST[k][:, t0:t0 + n], in_=ps, func=AF.Copy, scale=SCALE), deps=[tok, qst_free[k]])
                else:
                    te = S.op("dve", lambda e: e.tensor_scalar(out=QST[k][:, t0:t0 + n], in0=ps, scalar1=SCALE, scalar2=None, op0=ALU.mult),
                              deps=[tok, qst_free[k]])
                if tt == 2:
                    qst_free[k] = S.dma("sp", qt_loc.ap()[m * 128:(m + 1) * 128, :], QST[k][:], deps=[S.cur("act"), S.cur("dve")], sem=f"qst{k}")
                return te
            linear(3072 // BW, 16, lambda kc, t0, n: H[:, kc, t0:t0 + n], evac_q, th, lambda b, mi: b * CPB + mi)
            S.barrier()
            S.wait("pool", [t_agk, t_agv])
            S.wait("pool", [("d", nm, S.dcnt[nm]) for nm in ("tpz0", "tpz1", "qst0", "qst1")])

            if ATT_STAGE <= 2:
                return S.cur("dve")
            ATT = ARENA[:, 0:8 * NT].rearrange("p (h t) -> p h t", h=8)
            SLOT_EL = 11600
            so = AOFF1
            slots = []
            for k in range(2):
                b = so + k * SLOT_EL
                sl = {"KT": AV(b, 3072), "V": ARENA[:, b + 3072:b + 3072 + 4096].rearrange("p (n d) -> p n d", d=128),
                      "Q": AV(b + 7168, NT), "T0": AV(b + 8200, 128), "T1": AV(b + 8328, 128), "TS": AV(b + 8456, 8),
                      "KC": AV(b + 8464, 2048), "VC": ARENA[:, b + 10512:b + 10512 + 1024].rearrange("p (n d) -> p n d", d=128)}
                slots.append(sl)
            po = so + 2 * SLOT_EL
            PT = [AV(po + k * 128, 128) for k in range(4)]
            RCP = AVF(po + 512, 512)
            assert po + 512 + 1024 <= AEND - 3072 - 192
            slot_free = [None, None]
            pt_free = [None] * 4
            ucnt = [0]
            GWc = [128, 512, 2048]

            r1 = _R1
            r2 = _R2
            def flat(ap_, a=128):
                return ap_.rearrange("(a b) c -> a (b c)", a=a)
            tpv = []
            tpv.append(S.dma("pool", flat(kt_p1.ap()), flat(kt_out.ap()[bass.ds(r1 * 3072, 3072), :]), sem="prv"))
            tpv.append(S.dma("pool", flat(kt_p2.ap()), flat(kt_out.ap()[bass.ds(r2 * 3072 + 2048, 1024), :]), sem="prv"))
            tpv.append(S.dma("pool", flat(vt_p1.ap()), flat(vt_out.ap()[bass.ds(r1 * 24576, 24576), :]), sem="prv"))
            tpv.append(S.dma("pool", flat(vt_p2.ap()), flat(vt_out.ap()[bass.ds(r2 * 24576 + 16384, 8192), :]), sem="prv"))
            S.wait("pool", tpv[-1])
            S.wait("sp", tpv[-1])

            def load_slot(sl, g, hh, dep, si):
                toks = []
                d_ = [dep]
                row_l = (g * 8 + hh) * 128
                halo = [128, 512, 2048][g]
                toks.append(S.dma("sp", sl["KT"][:, halo:halo + NPR], kt_in3[row_l:row_l + 128, :], deps=d_, sem=f"sls{si}"))
                if g < 2:
                    src = kt_p1.ap()[row_l:row_l + 128, :]
                    toks.append(S.dma("sp", sl["KT"][:, 0:halo], src[:, NPR - halo:NPR], deps=d_, sem=f"sls{si}"))
                else:
                    src2 = kt_p2.ap()[hh * 128:hh * 128 + 128, :]
                    toks.append(S.dma("sp", sl["KT"][:, 0:NPR], src2, deps=d_, sem=f"sls{si}"))
                    src1 = kt_p1.ap()[row_l:row_l + 128, :]
                    toks.append(S.dma("sp", sl["KT"][:, NPR:2 * NPR], src1, deps=d_, sem=f"sls{si}"))
                vrow_l = (g * 8 + hh) * 1024
                vown = vt_in.ap()[vrow_l:vrow_l + 1024, :]
                def vprev(k):
                    if k == 1:
                        return vt_p1.ap()[vrow_l:vrow_l + 1024, :]
                    return vt_p2.ap()[hh * 1024:hh * 1024 + 1024, :]
                if g == 0:
                    toks.append(S.dma("sp", sl["V"][:, 1:9, :], vown.rearrange("(n p) d -> p n d", p=128), deps=d_, sem=f"sls{si}"))
                    toks.append(S.dma("sp", sl["V"][:, 0, :], vprev(1)[896:1024, :], deps=d_, sem=f"sls{si}"))
                elif g == 1:
                    toks.append(S.dma("sp", sl["V"][:, 4:12, :], vown.rearrange("(n p) d -> p n d", p=128), deps=d_, sem=f"sls{si}"))
                    toks.append(S.dma("sp", sl["V"][:, 0:4, :], vprev(1).rearrange("(r n p) d -> p r n d", r=4, n=2)[:, :, 1, :], deps=d_, sem=f"sls{si}"))
                else:
                    toks.append(S.dma("sp", sl["V"][0:64, 0:16, :], vprev(2).rearrange("(r p) d -> p r d", p=64), deps=d_, sem=f"sls{si}"))
                    toks.append(S.dma("sp", sl["V"][64:128, 0:16, :], vprev(1).rearrange("(r p) d -> p r d", p=64), deps=d_, sem=f"sls{si}"))
                    toks.append(S.dma("sp", sl["V"][0:64, 16:32, :], vown.rearrange("(r p) d -> p r d", p=64), deps=d_, sem=f"sls{si}"))
                toks.append(S.dma("sp", sl["Q"][:], qt_loc.ap()[row_l:row_l + 128, :], deps=d_, sem=f"sls{si}"))
                col = g * 8 + hh
                base = col * 128 * 400
                toks.append(S.dma("pool", sl["T0"][:], bass.AP(tpz, base + 128, [[399, 128], [1, 128]]), deps=d_, sem=f"slk{si}"))
                toks.append(S.dma("pool", sl["T1"][:], bass.AP(tpz, base + 256, [[399, 128], [1, 128]]), deps=d_, sem=f"slk{si}"))
                toks.append(S.dma("pool", sl["TS"][0:8, :], bass.AP(tpz, base + 391, [[399, 8], [1, 8]]), deps=d_, sem=f"slk{si}"))
                W = GWc[g]
                toks.append(S.dma("pool", sl["KC"][:, 0:W], kcT[g][hh], deps=d_, sem=f"slk{si}"))
                if g == 0:
                    toks.append(S.dma("pool", sl["VC"][:, 0, :], vcc[0][:, hh, :], deps=d_, sem=f"slk{si}"))
                elif g == 1:
                    toks.append(S.dma("pool", sl["VC"][:, 0:4, :], vcc[1].rearrange("(p r) h d -> p r h d", r=4)[:, :, hh, :], deps=d_, sem=f"slk{si}"))
                else:
                    toks.append(S.dma("pool", sl["VC"][:, 0:8, :], vcc[2].rearrange("(p r) h d -> p r h d", r=16)[:, 0:8, hh, :], deps=d_, sem=f"slk{si}"))
                return [("d", f"slk{si}", S.dcnt[f"slk{si}"]), ("d", f"sls{si}", S.dcnt[f"sls{si}"])]

            def unit(kt_ap, nk, v_ap, q_ap, nq, bias_ap, mask_ap, o_ap, d_ap, first, lastu, slot_tok, first_d=None):
                sb_ = 5 + (ucnt[0] % 3)
                pk = ucnt[0] % 4
                ucnt[0] += 1
                S.wait("pe", [bank_free[sb_], slot_tok])
                nc.tensor.matmul(PS[sb_][0:nk, 0:nq], lhsT=kt_ap, rhs=q_ap, start=True, stop=False)
                ins = nc.tensor.matmul(PS[sb_][0:nk, 0:nq], lhsT=ident_b[0:nk, 0:nk], rhs=bias_ap, start=False, stop=True)
                ts_ = S.mark("pe", ins)
                kw = {"bias": mask_ap} if mask_ap is not None else {}
                tp_ = S.op("act", lambda e: e.activation(out=PT[pk][0:nk, 0:nq], in_=PS[sb_][0:nk, 0:nq], func=AF.Exp, scale=1.0, **kw),
                           deps=[ts_, pt_free[pk]])
                bank_free[sb_] = tp_
                S.wait("pe", tp_)
                nc.tensor.matmul(o_ap, lhsT=v_ap, rhs=PT[pk][0:nk, 0:nq], start=first, stop=lastu)
                ins = nc.tensor.matmul(d_ap, lhsT=ones_b[0:nk, :], rhs=PT[pk][0:nk, 0:nq], start=(first if first_d is None else first_d), stop=lastu)
                pt_free[pk] = S.mark("pe", ins)
                return pt_free[pk]

            def qsl(ap, c0, n, step=1):
                return ap[:, c0:c0 + n] if step == 1 else ap[:, bass.ds(c0, n, step)]

            if ATT_STAGE <= 3:
                S.barrier()
                return S.cur("dve")
            od_free = None
            nxt = load_slot(slots[0], 0, 0, None, 0)
            for hh in range(8):
                for g in range(3):
                    idx = hh * 3 + g
                    sl = slots[idx % 2]
                    stok = nxt
                    if idx + 1 < 24:
                        nxt = load_slot(slots[(idx + 1) % 2], (idx + 1) % 3, (idx + 1) // 3, slot_free[(idx + 1) % 2], (idx + 1) % 2)
                    KT, V, Q = sl["KT"], sl["V"], sl["Q"]
                    T0, T1, TSt, KCt, VCt = sl["T0"], sl["T1"], sl["TS"], sl["KC"], sl["VC"]
                    if g == 0 and od_free is not None:
                        S.wait("pe", od_free)
                    def OD(c0, n, step=1):
                        tt = 0 if c0 < 512 else 1
                        cc = c0 - 512 * tt
                        if step == 1:
                            return PS[tt][:, cc:cc + n], PS[2 + tt][:, cc:cc + n]
                        return PS[tt][:, bass.ds(cc, n, step)], PS[2 + tt][:, bass.ds(cc, n, step)]
                    lt = None
                    if g == 0:
                        for qb in range(8):
                            oa, da = OD(qb * 128, 128)
                            q_ap = qsl(Q, qb * 128, 128)
                            unit(qsl(KT, 128 + qb * 128, 128), 128, V[:, 1 + qb, :], q_ap, 128, T0[:, :], None, oa, da, qb in (0, 4), False, stok)
                            mk = META[:, 1:2] if qb == 0 else None
                            lt = unit(qsl(KT, qb * 128, 128), 128, V[:, qb, :], q_ap, 128, T1[:, :], mk, oa, da, False, False, stok)
                    elif g == 1:
                        for r in range(4):
                            for n in range(2):
                                c0 = r + 512 * n
                                oa, da = OD(c0, 128, 4)
                                q_ap = qsl(Q, c0, 128, 4)
                                unit(qsl(KT, 512 + c0, 128, 4), 128, V[:, 4 + r * 2 + n, :], q_ap, 128, T0[:, :], None, oa, da, False, False, stok)
                                if n == 1:
                                    lt = unit(qsl(KT, 512 + r, 128, 4), 128, V[:, 4 + r * 2, :], q_ap, 128, T1[:, :], None, oa, da, False, False, stok)
                                else:
                                    lt = unit(qsl(KT, r, 128, 4), 128, V[:, r, :], q_ap, 128, T1[:, :], META[:, 1:2], oa, da, False, False, stok)
                    else:
                        for r in range(16):
                            for half in range(2):
                                c0 = r + 16 * 32 * half
                                oa, da = OD(c0, 32, 16)
                                q_ap = qsl(Q, c0, 32, 16)
                                unit(qsl(KT, r, 128, 16), 128, V[:, r, :], q_ap, 32, T1[:, 32 * half:32 * half + 32], META[:, 3:4], oa, da, False, False, stok)
                                lt = unit(qsl(KT, 2048 + r, 64, 16), 64, V[0:64, 16 + r, :], q_ap, 32, T0[0:64, 32 * half:32 * half + 32], None,
                                          oa, da, False, r == 15, stok)
                    def ODS(c0, n, step=1):
                        if step == 1:
                            return PS[4][:, c0:c0 + n], PS[4][:, 8 + c0:8 + c0 + n]
                        return PS[4][:, bass.ds(c0, n, step)], PS[4][:, bass.ds(8 + c0, n, step)]
                    if g == 0:
                        oa, da = ODS(0, 8)
                        unit(KCt[:, 0:128], 128, VCt[:, 0, :], Q[:, NPR:NT], 8, T1[:, 0:8], None, oa, da, True, False, stok, first_d=False)
                        lt = unit(KNS[:, 0, hh, :], 8, VNS[0:8, 0, hh, :], Q[:, NPR:NT], 8, TSt[0:8, :], None, oa, da, False, False, stok)
                    elif g == 1:
                        for rho in range(4):
                            oa, da = ODS(rho, 2, 4)
                            unit(qsl(KCt, rho, 128, 4), 128, VCt[:, rho, :], qsl(Q, NPR + rho, 2, 4), 2, T1[:, 0:2], None, oa, da, False, False, stok)
                        oa, da = ODS(0, 8)
                        lt = unit(KNS[:, 1, hh, :], 8, VNS[0:8, 1, hh, :], Q[:, NPR:NT], 8, TSt[0:8, :], None, oa, da, False, False, stok)
                    else:
                        for rho in range(8):
                            oa, da = ODS(rho, 1)
                            unit(qsl(KCt, rho, 128, 16), 128, VCt[:, rho, :], Q[:, NPR + rho:NPR + rho + 1], 1, T1[:, 0:1], None, oa, da, False, False, stok)
                        oa, da = ODS(0, 8)
                        lt = unit(KNS[:, 2, hh, :], 8, VNS[0:8, 2, hh, :], Q[:, NPR:NT], 8, TSt[0:8, :], None, oa, da, False, True, stok)
                    slot_free[idx % 2] = lt
                for tt in range(3):
                    t0, n = TT[tt]
                    if tt < 2:
                        o_src, d_src = PS[tt][:, 0:n], PS[2 + tt][:, 0:n]
                    else:
                        o_src, d_src = PS[4][:, 0:8], PS[4][:, 8:16]
                    tr = S.op("dve", lambda e, d_src=d_src, n=n: e.reciprocal(out=RCP[:, 0:n], in_=d_src), deps=[lt, S.cur("dve")])
                    od_free = S.op("dve", lambda e, o_src=o_src, n=n, t0=t0: e.tensor_tensor(out=ATT[:, hh, t0:t0 + n], in0=o_src, in1=RCP[:, 0:n],
                                                                                          op=ALU.mult), deps=[tr])
            t_att = od_free
            for bk in range(8): bank_free[bk] = t_att

            def evac_x(m, tt, ps, tok):
                t0, n = TT[tt]
                return S.op("dve", lambda e: e.tensor_tensor(out=X[:, m, t0:t0 + n], in0=X[:, m, t0:t0 + n], in1=ps, op=ALU.add),
                            deps=[tok])
            linear(2048 // BW, 8, lambda kc, t0, n: ATT[:, kc, t0:t0 + n], evac_x, [t_att], lambda b, mi: b * CPB + mi)
            return S.cur("dve")

        x_tok = S.cur("dve")
        for i in range(START_LAYER, NLAYERS):
            kind = i % 3
            if kind == 0:
                j = i // 3
                x_tok = conv_layer(i, j, x_tok, sc0T if j == 0 else sc3T, o_c0p if j == 0 else o_c3p,
                                   o_c0s if j == 0 else o_c3s, j)
                S.barrier()
            elif kind == 1:
                x_tok = lru_layer(i, x_tok)
                S.barrier()
            else:
                x_tok = att_layer(i, x_tok)
                S.barrier()
            x_tok = mlp(i, x_tok)
            S.barrier()

        rmsnorm(8, x_tok, X)
        S.barrier()
        S.dma("sp", yT.rearrange("(kc p) t -> p kc t", p=128), X[:], sem="out")
        S.barrier()
    return nc


_CACHE = {}


def _pack_pvec(inp):
    vecs = []
    for i in range(4): vecs.append(inp["norm_mix_g"][i])
    for i in range(4): vecs.append(inp["norm_ffn_g"][i])
    vecs.append(inp["final_norm_g"])
    for j in range(2):
        for t in range(CW): vecs.append(inp["conv_dw_w"][j, t])
        vecs += [inp["conv_dw_b"][j], inp["conv_ln_g"][j], inp["conv_ln_b"][j]]
    for t in range(4): vecs.append(inp["lru_conv_w"][t])
    vecs += [inp["lru_conv_b"], inp["lru_b_a"], inp["lru_b_x"], inp["lru_lambda"]]
    a = np.stack([np.asarray(v, np.float32) for v in vecs])
    return np.ascontiguousarray(a.reshape(NPV, KC, 128).transpose(2, 0, 1))


def _t5_bucket(dist):
    n_buckets, max_distance = 32, 2048
    max_exact = n_buckets // 2
    d = np.asarray(dist, dtype=np.int32)
    large = max_exact + (np.log(np.maximum(d, max_exact) / max_exact) / np.log(max_distance / max_exact)
                         * (n_buckets - max_exact)).astype(np.int32)
    return np.where(d < max_exact, d, np.minimum(large, n_buckets - 1)).astype(np.int32)


def _sel_const():
    sel = np.zeros((33, 3, 400), np.float32)
    for g, dil in enumerate((1, 4, 16)):
        for i in range(400):
            b = None
            if i < 384:
                delta = i - 128
                if 0 <= delta <= 128:
                    b = int(_t5_bucket(delta * dil))
            elif i < 399:
                j = i - 391
                if j >= 0 and j % dil == 0:
                    b = int(_t5_bucket(j))
            sel[32 if b is None else b, g, i] = 1.0
    return sel


def kernel(**inp):
    inp = {k: np.asarray(v) for k, v in inp.items()}
    if "nc" not in _CACHE:
        _CACHE["nc"] = build_program()
    nc = _CACHE["nc"]
    pv = _pack_pvec(inp)
    xp, xs = inp["x_prompt"], inp["x_sample"]
    in_maps = []
    for c in range(NCORES):
        b, ch = c // 4, c % 4
        xt = np.concatenate([xp[b, ch * NPR:(ch + 1) * NPR], xs[c]], axis=0)
        meta = np.zeros((128, 16), np.float32)
        in_maps.append({
            "xT": np.ascontiguousarray(xt.T),
            "pvec": pv,
            "meta": meta,
            "sc0T": np.ascontiguousarray(inp["state_conv_l0"][c].T),
            "sc3T": np.ascontiguousarray(inp["state_conv_l3"][c].T),
        })
        m = in_maps[-1]
        lo = (c // 4) * 4
        for r in range(NCORES):
            meta[:, 4 + r] = 1.0 if (lo <= r < c) else 0.0
        ch_ = c % 4
        meta[:, 1] = 0.0 if ch_ >= 1 else -30000.0
        meta[:, 2] = 0.0 if ch_ >= 2 else -30000.0
        meta[0:64, 3] = meta[0, 2]
        meta[64:128, 3] = meta[0, 1]
        if NLAYERS >= 3 and START_LAYER <= 2:
            m["att_w_qkv"] = inp["att_w_qkv"]; m["att_w_o"] = inp["att_w_o"]; m["rel_bias"] = inp["rel_bias"]
            m["selc"] = _CACHE.setdefault("sel", _sel_const())
            for g, nm in enumerate(("cache_kv_w128_l2", "cache_kv_w512_l2", "cache_kv_w2048_l2")):
                cch = inp[nm][c]
                m[f"kcT{g}"] = np.ascontiguousarray(cch[:, 0].transpose(1, 2, 0))
                m[f"vcc{g}"] = np.ascontiguousarray(cch[:, 1])
        if NLAYERS >= 2 and START_LAYER <= 1:
            m["lru_w_in"] = inp["lru_w_in"]; m["lru_w_a"] = inp["lru_w_a"]; m["lru_w_x"] = inp["lru_w_x"]
            m["lru_w_out"] = inp["lru_w_out"]
            m["lh0"] = np.ascontiguousarray(inp["state_lru_h_l1"][c].reshape(KC, 128).T)
            m["lcsT"] = np.ascontiguousarray(inp["state_lru_conv_l1"][c].T)
        for i in range(START_LAYER, NLAYERS):
            m[f"ffn_w1_{i}"] = inp["ffn_w1"][i]
            m[f"ffn_w2_{i}"] = inp["ffn_w2"][i]
            if i % 3 == 0:
                m[f"conv_w_glu_{i // 3}"] = inp["conv_w_glu"][i // 3]
                m[f"conv_w_out_{i // 3}"] = inp["conv_w_out"][i // 3]
    ncr = int(os.environ.get("DEBUG_CORES", NCORES))
    if START_LAYER > 0:
        for m in in_maps:
            m.pop("sc0T", None)
            if START_LAYER > 0 and NLAYERS < 4: m.pop("sc3T", None)
    res = run_bass_kernel_spmd(nc, in_maps[:ncr], core_ids=list(range(ncr)))
    R = list(res.results) + [res.results[0]] * (NCORES - ncr)
    if START_LAYER > 0:
        return R
    B, SEQ = xp.shape[0], xp.shape[1]
    y_prompt = np.zeros((B, SEQ, D), np.float32)
    y_sample = np.zeros((NCORES, NSM, D), np.float32)
    for c in range(NCORES):
        yt = R[c]["yT"].T
        y_prompt[c // 4, (c % 4) * NPR:(c % 4 + 1) * NPR] = yt[:NPR]
        y_sample[c] = yt[NPR:]
    def stT(name, cores): return np.stack([R[c][name].T for c in cores]).astype(np.float32)
    z = lambda *s: np.zeros(s, np.float32)
    H, DH = 8, 128
    outs = (y_prompt, y_sample,
            stT("o_c0p", [3, 7]), stT("o_c0s", range(8)),
            *((np.stack([R[c]["o_lhp"].T.reshape(D) for c in (3, 7)]), np.stack([R[c]["o_lhs"].T.reshape(D) for c in range(8)]),
               stT("o_lcp", [3, 7]), stT("o_lcs", range(8))) if NLAYERS >= 2 else (z(B, D), z(8, D), z(B, 3, D), z(8, 3, D))),
            *((np.stack([R[c]["ok_g0"] for c in (3, 7)]), np.stack([R[c]["osn"][0] for c in range(8)]),
               np.stack([R[c]["ok_g1"] for c in (3, 7)]), np.stack([R[c]["osn"][1] for c in range(8)]),
               np.stack([np.concatenate([R[c]["ok_g2"], R[c + 1]["ok_g2"]], axis=0) for c in (2, 6)]),
               np.stack([R[c]["osn"][2] for c in range(8)])) if NLAYERS >= 3 else
              (z(B, 128, 2, H, DH), z(8, 8, 2, H, DH), z(B, 512, 2, H, DH), z(8, 8, 2, H, DH),
               z(B, 2048, 2, H, DH), z(8, 8, 2, H, DH))),
            stT("o_c3p", [3, 7]), stT("o_c3s", range(8)))
    return outs
```
